# Optimizing a Trainium2 kernel written in Bass

```python
import jax, jax.numpy as jnp
from jax import lax
import numpy as np

D_MODEL = 1024
BATCH = 8
SEQ = 2048
DEPTH = 4

CHUNK = 128
A_WIDTH = D_MODEL
A_HEADS = 8
A_HEAD_DIM = A_WIDTH // A_HEADS
B_WIDTH = D_MODEL
B_GROUPS = 16
B_GROUP_DIM = B_WIDTH // B_GROUPS
CONV_WIDTH = 3
D_FF = ((8 * D_MODEL // 3 + 127) // 128) * 128
N_IN = 2 * A_WIDTH + 3 * B_WIDTH + 2 * D_MODEL
EPS = 1e-6

kernel_name = "macaron_gmlp_shortconv_gated_hybrid"


def rmsnorm(x, g):
    x32 = x.astype(jnp.float32)
    y = x32 * lax.rsqrt(jnp.mean(x32 * x32, axis=-1, keepdims=True) + EPS)
    return y.astype(x.dtype) * g


def swiglu(h, w_in, w_out):
    gate, up = jnp.split(h @ w_in, 2, axis=-1)
    return (jax.nn.silu(gate) * up) @ w_out


def spatial_gating(u, v, norm_g, w_s, b_s):
    bsz, s, _ = v.shape
    v = rmsnorm(v, norm_g)
    v = v.reshape(bsz, s // CHUNK, CHUNK, A_HEADS, A_HEAD_DIM)
    v = jnp.einsum('hqp,bcphd->bcqhd', w_s, v) + b_s.T[None, None, :, :, None]
    return u * v.reshape(bsz, s, A_WIDTH)


def short_conv(t, w):
    s = t.shape[1]
    pad = CONV_WIDTH // 2
    tp = jnp.pad(t, ((0, 0), (pad, CONV_WIDTH - 1 - pad), (0, 0)))
    out = w[0] * tp[:, 0:s]
    for k in range(1, CONV_WIDTH):
        out = out + w[k] * tp[:, k:k + s]
    return out


def hybrid_mixer(h, w_in, sgu_norm, sgu_w, sgu_b, conv_w, w_proj_a, w_proj_b, w_out):
    z = h @ w_in
    za, zb, zg = jnp.split(z, [2 * A_WIDTH, 2 * A_WIDTH + 3 * B_WIDTH], axis=-1)
    u, v = jnp.split(jax.nn.gelu(za), 2, axis=-1)
    y_a = spatial_gating(u, v, sgu_norm, sgu_w, sgu_b) @ w_proj_a
    gate_b, gate_c, xb = jnp.split(zb, 3, axis=-1)
    y_b = (gate_b * short_conv(gate_c * xb, conv_w)) @ w_proj_b
    g_a, g_b = jnp.split(zg, 2, axis=-1)
    merged = jax.nn.sigmoid(g_a) * y_a + jax.nn.sigmoid(g_b) * y_b
    return merged @ w_out


def setup_inputs(seed: int = 0) -> dict:
    key = jax.random.key(seed)
    ks = jax.random.split(key, 20)
    f32 = jnp.float32

    def nrm(k, shape, scale):
        return jax.random.normal(k, shape, f32) * scale

    def gain(k, shape):
        return 1.0 + 0.05 * jax.random.normal(k, shape, f32)

    return {
        "x": nrm(ks[0], (BATCH, SEQ, D_MODEL), 1.0),
        "ffn1_norm": gain(ks[1], (DEPTH, D_MODEL)),
        "ffn1_w_in": nrm(ks[2], (DEPTH, D_MODEL, 2 * D_FF), D_MODEL ** -0.5),
        "ffn1_w_out": nrm(ks[3], (DEPTH, D_FF, D_MODEL), D_FF ** -0.5),
        "mix_norm": gain(ks[4], (DEPTH, D_MODEL)),
        "w_in": nrm(ks[5], (DEPTH, D_MODEL, N_IN), D_MODEL ** -0.5),
        "sgu_norm": gain(ks[6], (DEPTH, A_WIDTH)),
        "sgu_w": nrm(ks[7], (DEPTH, A_HEADS, CHUNK, CHUNK), CHUNK ** -0.5),
        "sgu_b": gain(ks[8], (DEPTH, A_HEADS, CHUNK)),
        "conv_w": nrm(ks[9], (DEPTH, CONV_WIDTH, B_WIDTH), CONV_WIDTH ** -0.5),
        "w_proj_a": nrm(ks[10], (DEPTH, A_WIDTH, D_MODEL), A_WIDTH ** -0.5),
        "w_proj_b": nrm(ks[11], (DEPTH, B_WIDTH, D_MODEL), B_WIDTH ** -0.5),
        "w_out": nrm(ks[12], (DEPTH, D_MODEL, D_MODEL), D_MODEL ** -0.5),
        "ffn2_norm": gain(ks[13], (DEPTH, D_MODEL)),
        "ffn2_w_in": nrm(ks[14], (DEPTH, D_MODEL, 2 * D_FF), D_MODEL ** -0.5),
        "ffn2_w_out": nrm(ks[15], (DEPTH, D_FF, D_MODEL), D_FF ** -0.5),
        "final_norm": gain(ks[16], (D_MODEL,)),
    }


def reference(x, ffn1_norm, ffn1_w_in, ffn1_w_out, mix_norm, w_in, sgu_norm, sgu_w,
              sgu_b, conv_w, w_proj_a, w_proj_b, w_out, ffn2_norm, ffn2_w_in,
              ffn2_w_out, final_norm):
    for l in range(DEPTH):
        x = x + 0.5 * swiglu(rmsnorm(x, ffn1_norm[l]), ffn1_w_in[l], ffn1_w_out[l])
        x = x + hybrid_mixer(rmsnorm(x, mix_norm[l]), w_in[l], sgu_norm[l], sgu_w[l],
                             sgu_b[l], conv_w[l], w_proj_a[l], w_proj_b[l], w_out[l])
        x = x + 0.5 * swiglu(rmsnorm(x, ffn2_norm[l]), ffn2_w_in[l], ffn2_w_out[l])
    return rmsnorm(x, final_norm)
```

```python
import numpy as np
from contextlib import ExitStack

import concourse.bass as bass
import concourse.mybir as mybir
from concourse.bass_utils import run_bass_kernel_spmd

F32 = mybir.dt.float32
BF16 = mybir.dt.bfloat16
AF = mybir.ActivationFunctionType
ALU = mybir.AluOpType

D = 1024
SEQ = 2048
DFF = 2816
NIN = 7168
HALF = 1024
EPS = 1e-6
DEPTH = 4
NCORES = 8
RING = 14
NV = DEPTH * 56 + 8


class Sched:
    def __init__(self, nc, stack):
        self.nc = nc
        self.stack = stack
        self.q = {e: [] for e in ("pe", "act", "dve", "pool", "sp")}
        self.sem = {}
        self.cnt = {}
        self.waited = {}
        self.lastw = {}
        self.readers = {}
        self.dry = False
        self.own = {}
        for e in ("pe", "act", "dve"):
            self.own[e] = self.newsem("c_" + e)

    def newsem(self, name):
        self.sem[name] = self.stack.enter_context(self.nc.semaphore(name))
        self.cnt[name] = 0
        return name

    def _deps(self, eng, reads, writes):
        need = {}
        for k in reads:
            ev = self.lastw.get(k)
            if ev is not None:
                need[ev[0]] = max(need.get(ev[0], 0), ev[1])
        for k in writes:
            ev = self.lastw.get(k)
            if ev is not None:
                need[ev[0]] = max(need.get(ev[0], 0), ev[1])
            for ev in self.readers.get(k, ()):
                need[ev[0]] = max(need.get(ev[0], 0), ev[1])
        waits = []
        for s, v in need.items():
            if self.waited.get((eng, s), 0) < v:
                waits.append((s, v))
                self.waited[(eng, s)] = v
        return waits

    def _record(self, ev, reads, writes):
        for k in writes:
            self.lastw[k] = ev
            self.readers[k] = []
        for k in reads:
            if k[0] == "const":
                continue
            self.readers.setdefault(k, []).append(ev)

    def op(self, eng, fn, reads=(), writes=(), dma_sem=None):
        if self.dry:
            return None
        waits = self._deps(eng, reads, writes)
        if dma_sem is None:
            s = self.own[eng]
            self.cnt[s] += 1
            inc = (s, 1)
        else:
            s = dma_sem
            self.cnt[s] += 16
            inc = (s, 16)
        ev = (s, self.cnt[s])
        self._record(ev, reads, writes)
        self.q[eng].append((waits, fn, inc))
        return ev

    def pe_job(self, mms, bank_key):
        if self.dry:
            return None
        s = self.own["pe"]
        self.cnt[s] += 1
        ev = (s, self.cnt[s])
        allr = []
        n = len(mms)
        for i, (fn, reads) in enumerate(mms):
            waits = self._deps("pe", reads, [bank_key] if i == 0 else [])
            self.q["pe"].append((waits, fn, (s, 1) if i == n - 1 else None))
            allr.extend(reads)
        self._record(ev, set(allr), [bank_key])
        return ev

    def emit(self, eng_name, eng):
        for waits, fn, inc in self.q[eng_name]:
            for s, v in waits:
                eng.wait_ge(self.sem[s], v)
            ins = fn(eng)
            if inc is not None:
                ins.then_inc(self.sem[inc[0]], inc[1])


def build_nc(depth=DEPTH):
    nc = bass.Bass("TRN2", target_bir_lowering=False)

    def din(name, shape):
        return nc.dram_tensor(name, list(shape), F32, kind="ExternalInput").ap()

    xT = din("xT", [D, SEQ])
    vecs_d = din("vecs", [128, NV])
    bb_d = din("bb", [DEPTH, 128, 1024])
    wst_d = din("wst", [DEPTH, 128, 8, 128])
    W = {
        "ffn1_w_in": din("ffn1_w_in", [DEPTH, D, 2 * DFF]),
        "ffn1_w_out": din("ffn1_w_out", [DEPTH, DFF, D]),
        "w_in": din("w_in", [DEPTH, D, NIN]),
        "w_proj_a": din("w_proj_a", [DEPTH, D, D]),
        "w_proj_b": din("w_proj_b", [DEPTH, D, D]),
        "w_out": din("w_out", [DEPTH, D, D]),
        "ffn2_w_in": din("ffn2_w_in", [DEPTH, D, 2 * DFF]),
        "ffn2_w_out": din("ffn2_w_out", [DEPTH, DFF, D]),
    }
    yT = nc.dram_tensor("yT", [D, SEQ], F32, kind="ExternalOutput").ap()

    with ExitStack() as stack:
        def sb(name, shape, dt):
            return stack.enter_context(nc.sbuf_tensor(name, list(shape), dt))

        x = sb("x", [128, 8, SEQ], F32)
        hb = sb("hb", [128, 8, HALF + 8], BF16)
        S = sb("S", [128, 24, HALF], BF16)
        rstd = sb("rstd", [128, HALF], F32)
        ring = sb("ring", [128, RING, 1024], BF16)
        tmpg = sb("tmpg", [128, 2, 512], BF16)
        u_sb = sb("u_sb", [128, 1024], F32)
        tmps = sb("tmps", [128, 512], F32)
        tbuf = sb("tbuf", [128, 1026], F32)
        acc = sb("acc", [128, 1024], F32)
        sg = sb("sg", [128, 4, 512], F32)
        vecs = sb("vecs_sb", [128, NV], F32)
        Bb = sb("Bb", [128, 1024], F32)
        wsT = sb("wsT", [128, 8, 128], BF16)
        ones = sb("ones", [128, 128], BF16)
        ss = sb("ss", [128, 8], F32)
        rs = sb("rs", [128, 8], F32)
        sqh = sb("sqh", [128, 8], BF16)
        tl = sb("tl", [128, 8], F32)
        rsh = sb("rsh", [128, 1], F32)
        epsc = sb("epsc", [128, 1], F32)
        ps = stack.enter_context(nc.psum_tensor("ps", [128, 8, 512], F32))

        sc = Sched(nc, stack)
        for s_ in range(RING):
            sc.newsem("r%d" % s_)
        for k in range(16):
            sc.newsem("xl%d" % k)
        sc.newsem("vl")
        sc.newsem("bbs")
        sc.newsem("wss")
        sc.newsem("outs")

        st = {"bank": 0, "tiles": [], "ti": 0, "next_dma": 0, "released": set()}

        def newbank():
            b = st["bank"] % 8
            st["bank"] += 1
            return b

        act = lambda fn, r=(), w=(): sc.op("act", fn, r, w)
        dve = lambda fn, r=(), w=(): sc.op("dve", fn, r, w)

        def issue_dma(m):
            spec = st["tiles"][m]
            s_ = m % RING
            if spec[0] == "col":
                _, name, l, r0, nk, c0 = spec
                src = W[name][l, r0:r0 + nk * 128, c0:c0 + 128].rearrange("(k p) n -> p k n", p=128)
                dst = ring[:, s_, 0:nk * 128].rearrange("p (k n) -> p k n", k=nk)
            else:
                _, name, l, r0, c0 = spec
                src = W[name][l, r0:r0 + 128, c0:c0 + 1024]
                dst = ring[:, s_, :]
            sc.op("pool", lambda e, dst=dst, src=src: e.dma_start(out=dst, in_=src),
                  reads=(), writes=[("ring", s_)], dma_sem="r%d" % s_)

        def pump():
            if sc.dry:
                return
            n = len(st["tiles"])
            while st["next_dma"] < n:
                m = st["next_dma"]
                if m >= RING and (m - RING) not in st["released"]:
                    break
                issue_dma(m)
                st["next_dma"] += 1

        def tile(spec):
            if sc.dry:
                st["tiles"].append(spec)
                return 0
            m = st["ti"]
            assert st["tiles"][m] == spec
            st["ti"] += 1
            assert m < st["next_dma"], "tile not prefetched (ring too small?)"
            return m % RING, m

        def release(t):
            if sc.dry:
                return
            st["released"].add(t[1])
            pump()

        def col(name, l, r0, nk, c0):
            return tile(("col", name, l, r0, nk, c0))

        def zjob(t_, tt, nk=8):
            b = newbank()
            if sc.dry:
                return b
            s_ = t_[0]
            mms = []
            for k in range(nk):
                mms.append((lambda e, k=k, b=b, s_=s_, tt=tt: e.matmul(
                    ps[:, b, :], lhsT=ring[:, s_, k * 128:(k + 1) * 128],
                    rhs=hb[:, k, tt * 512:(tt + 1) * 512], start=(k == 0), stop=(k == nk - 1)),
                    [("ring", s_), ("h", k, tt)]))
            sc.pe_job(mms, ("ps", b))
            return b

        def zjobw(t_, off, w):
            b = newbank()
            if sc.dry:
                return b
            s_ = t_[0]
            tset = sorted(set(min(cc // 512, 2) for cc in (off, off + w - 1)))
            mms = []
            for k in range(8):
                mms.append((lambda e, k=k, b=b, s_=s_: e.matmul(
                    ps[:, b, 0:w], lhsT=ring[:, s_, k * 128:(k + 1) * 128],
                    rhs=hb[:, k, off:off + w], start=(k == 0), stop=(k == 7)),
                    [("ring", s_)] + [("h", k, tt) for tt in tset]))
            sc.pe_job(mms, ("ps", b))
            return b

        def sjob(t_, tt, rows, nk=8):
            b = newbank()
            if sc.dry:
                return b
            s_ = t_[0]
            mms = []
            for k in range(nk):
                mms.append((lambda e, k=k, b=b, s_=s_, tt=tt: e.matmul(
                    ps[:, b, :], lhsT=ring[:, s_, k * 128:(k + 1) * 128],
                    rhs=S[:, rows[k], tt * 512:(tt + 1) * 512], start=(k == 0), stop=(k == nk - 1)),
                    [("ring", s_), ("S", rows[k], tt)]))
            sc.pe_job(mms, ("ps", b))
            return b

        def xkeys(k, half):
            return [("x", k, 2 * half), ("x", k, 2 * half + 1)]

        def hkeys(k):
            return [("h", k, 0), ("h", k, 1)]

        def norm_gen(vcol, half, final=False):
            c0 = half * HALF
            for k in range(8):
                act(lambda e, k=k: e.activation(out=hb[:, k, 0:HALF], in_=x[:, k, c0:c0 + HALF], func=AF.Square),
                    xkeys(k, half), hkeys(k))
            yield
            for t in range(2):
                b = newbank()
                if not sc.dry:
                    mms = [(lambda e, k=k, b=b, t=t: e.matmul(
                        ps[:, b, :], lhsT=ones[:, :], rhs=hb[:, k, t * 512:(t + 1) * 512],
                        start=(k == 0), stop=(k == 7)), [("h", k, t), ("const", "ones")]) for k in range(8)]
                    sc.pe_job(mms, ("ps", b))
                act(lambda e, b=b, t=t: e.activation(out=rstd[:, t * 512:(t + 1) * 512], in_=ps[:, b, :],
                                                     func=AF.Sqrt, bias=epsc[:, 0:1]), [("ps", b), ("const", "eps")],
                    [("rstd", t)])
                dve(lambda e, t=t: e.reciprocal(out=rstd[:, t * 512:(t + 1) * 512], in_=rstd[:, t * 512:(t + 1) * 512]),
                    [("rstd", t)], [("rstd", t)])
            yield
            for k in range(8):
                if k in (2, 4, 6):
                    yield
                if final:
                    out, wk = x[:, k, c0:c0 + HALF], xkeys(k, half)
                else:
                    out, wk = hb[:, k, 0:HALF], hkeys(k)
                dve(lambda e, k=k, out=out: e.scalar_tensor_tensor(
                    out=out, in0=x[:, k, c0:c0 + HALF], scalar=vecs[:, vcol + k:vcol + k + 1],
                    in1=rstd[:, :], op0=ALU.mult, op1=ALU.mult),
                    xkeys(k, half) + [("rstd", 0), ("rstd", 1), ("const", "vecs")], wk)

        def step(g):
            if g is not None:
                next(g, None)

        def drain(g):
            if g is not None:
                for _ in g:
                    pass

        def ffn_pass(l, which, half, own, nxt):
            c0 = half * HALF
            win = "ffn%d_w_in" % which
            wout = "ffn%d_w_out" % which
            drain(own)
            groups = [(0, 8), (8, 6), (14, 8)]
            slot = 0
            for gi, (j0, nj) in enumerate(groups):
                ab = (gi % 2) * 8
                for jj in range(nj):
                    j = j0 + jj
                    tg = col(win, l, 0, 8, j * 128)
                    tu = col(win, l, 0, 8, DFF + j * 128)
                    for t in range(2):
                        bg = zjob(tg, t)
                        bu = zjob(tu, t)
                        sl = slot % 2
                        slot += 1
                        act(lambda e, bg=bg, sl=sl: e.activation(out=tmpg[:, sl, :], in_=ps[:, bg, :], func=AF.Silu),
                            [("ps", bg)], [("tmpg", sl)])
                        dve(lambda e, bu=bu, sl=sl, r=ab + jj, t=t: e.tensor_tensor(
                            out=S[:, r, t * 512:(t + 1) * 512], in0=ps[:, bu, :], in1=tmpg[:, sl, :], op=ALU.mult),
                            [("ps", bu), ("tmpg", sl)], [("S", ab + jj, t)])
                    release(tg)
                    release(tu)
                for c in range(8):
                    if gi == 2 and c in (0, 2, 3, 4, 5, 6):
                        step(nxt)
                    tw = col(wout, l, j0 * 128, nj, c * 128)
                    for t in range(2):
                        b = sjob(tw, t, [ab + jj for jj in range(nj)], nk=nj)
                        xs = x[:, c, c0 + t * 512:c0 + (t + 1) * 512]
                        dve(lambda e, b=b, xs=xs: e.scalar_tensor_tensor(
                            out=xs, in0=ps[:, b, :], scalar=0.5, in1=xs, op0=ALU.mult, op1=ALU.add),
                            [("ps", b), ("x", c, 2 * half + t)], [("x", c, 2 * half + t)])
                    release(tw)
            drain(nxt)

        def mixer_pass(l, half, own, nxt):
            c0 = half * HALF
            vb = l * 56
            drain(own)

            def halo_norm():
                xc = x[:, :, HALF:HALF + 1].rearrange("p k o -> p (k o)")
                xck = [("x", k, 2) for k in range(8)]
                dve(lambda e: e.tensor_tensor(out=sqh[:, :], in0=xc, in1=xc, op=ALU.mult), xck, [("sqh",)])
                b = newbank()
                if not sc.dry:
                    mms = [(lambda e, k=k, b=b: e.matmul(ps[:, b, 0:1], lhsT=ones[:, :], rhs=sqh[:, k:k + 1],
                                                         start=(k == 0), stop=(k == 7)),
                            [("sqh",), ("const", "ones")]) for k in range(8)]
                    sc.pe_job(mms, ("ps", b))
                act(lambda e, b=b: e.activation(out=rsh[:, :], in_=ps[:, b, 0:1], func=AF.Sqrt, bias=epsc[:, 0:1]),
                    [("ps", b), ("const", "eps")], [("rsh",)])
                dve(lambda e: e.reciprocal(out=rsh[:, :], in_=rsh[:, :]), [("rsh",)], [("rsh",)])
                dve(lambda e: e.scalar_tensor_tensor(
                    out=hb[:, :, HALF:HALF + 1].rearrange("p k o -> p (k o)"), in0=xc, scalar=rsh[:, 0:1],
                    in1=vecs[:, vb + 8:vb + 16], op0=ALU.mult, op1=ALU.mult),
                    xck + [("rsh",), ("const", "vecs")], [("h", k, 2) for k in range(8)])
                dve(lambda e: e.memset(tbuf[:, 0:1], 0.0), [], [("tbufL",)])

            wv = [tile(("row", "w_in", l, k * 128, 1024)) for k in range(8)]
            for i in range(8):
                if i == 4:
                    if half == 0:
                        halo_norm()
                    else:
                        dve(lambda e: e.memset(tbuf[:, 1025:1026], 0.0), [], [("tbuf", "E")])
                for fh in range(2):
                    b = newbank()
                    if not sc.dry:
                        mms = [(lambda e, k=k, b=b, i=i, fh=fh, s_=wv[k][0]: e.matmul(
                            ps[:, b, :], lhsT=hb[:, k, i * 128:(i + 1) * 128],
                            rhs=ring[:, s_, fh * 512:(fh + 1) * 512], start=(k == 0), stop=(k == 7)),
                            [("ring", wv[k][0]), ("h", k, i // 4)]) for k in range(8)]
                        sc.pe_job(mms, ("ps", b))
                    act(lambda e, b=b, fh=fh, i=i: e.activation(out=S[:, i, fh * 512:(fh + 1) * 512], in_=ps[:, b, :],
                                                                func=AF.Gelu_apprx_tanh), [("ps", b)], [("S", i, fh)])
                dve(lambda e, i=i: e.scalar_tensor_tensor(
                    out=tmpg[:, :, :].rearrange("p a n -> p (a n)"), in0=S[:, i, :], scalar=1.0, in1=S[:, i, :],
                    op0=ALU.mult, op1=ALU.mult, accum_out=ss[:, i:i + 1]),
                    [("S", i, 0), ("S", i, 1), ("ss",)], [("tmpg", 0), ("tmpg", 1), ("ss",)])
            act(lambda e: e.activation(out=rs[:, :], in_=ss[:, :], func=AF.Sqrt, bias=epsc[:, 0:1],
                                       scale=1.0 / 1024.0), [("ss",), ("const", "eps")], [("rs",)])
            dve(lambda e: e.reciprocal(out=rs[:, :], in_=rs[:, :]), [("rs",)], [("rs",)])
            for i in range(8):
                dve(lambda e, i=i: e.tensor_scalar(out=S[:, i, :], in0=S[:, i, :], scalar1=rs[:, i:i + 1],
                                                   scalar2=None, op0=ALU.mult),
                    [("S", i, 0), ("S", i, 1), ("rs",)], [("S", i, 0), ("S", i, 1)])
            for t_ in wv:
                release(t_)

            for c in range(8):
                tu = col("w_in", l, 0, 8, c * 128)
                for t in range(2):
                    b = zjob(tu, t)
                    act(lambda e, b=b, t=t: e.activation(out=u_sb[:, t * 512:(t + 1) * 512], in_=ps[:, b, :],
                                                         func=AF.Gelu_apprx_tanh), [("ps", b)], [("u", t)])
                release(tu)
                for t in range(2):
                    b = newbank()
                    if not sc.dry:
                        mms = [(lambda e, ii=ii, b=b, t=t, c=c: e.matmul(
                            ps[:, b, ii * 128:(ii + 1) * 128], lhsT=S[:, 4 * t + ii, c * 128:(c + 1) * 128],
                            rhs=wsT[:, c, :], start=True, stop=True),
                            [("S", 4 * t + ii, c // 4), ("wsT",)]) for ii in range(4)]
                        sc.pe_job(mms, ("ps", b))
                    dve(lambda e, b=b, c=c: e.scalar_tensor_tensor(
                        out=tmps[:, :].rearrange("p (a q) -> p a q", a=4),
                        in0=ps[:, b, :].rearrange("p (a q) -> p a q", a=4),
                        scalar=vecs[:, vb + 16 + c:vb + 17 + c],
                        in1=Bb[:, c * 128:(c + 1) * 128].unsqueeze(1).broadcast_to([128, 4, 128]),
                        op0=ALU.mult, op1=ALU.add),
                        [("ps", b), ("Bb",), ("const", "vecs")], [("tmps",)])
                    dve(lambda e, c=c, t=t: e.tensor_tensor(
                        out=S[:, 8 + c, t * 512:(t + 1) * 512], in0=tmps[:, :], in1=u_sb[:, t * 512:(t + 1) * 512],
                        op=ALU.mult), [("tmps",), ("u", t)], [("S", 8 + c, t)])
                tgc = col("w_in", l, 0, 8, 3072 + c * 128)
                txb = col("w_in", l, 0, 8, 4096 + c * 128)
                if half == 0:
                    wins = [(0, 342, ["A"]), (342, 342, ["B", "C"]), (684, 341, ["D", "E"])]
                else:
                    wins = [(0, 512, ["A", "B"]), (512, 512, ["C", "D"])]
                for j, (off, w, segs) in enumerate(wins):
                    bgc = zjobw(tgc, off, w)
                    bxb = zjobw(txb, off, w)
                    act(lambda e, b=bgc, j=j, w=w: e.activation(out=sg[:, j, 0:w], in_=ps[:, b, 0:w], func=AF.Copy),
                        [("ps", bgc)], [("sg", j)])
                    dve(lambda e, b=bxb, j=j, off=off, w=w: e.tensor_tensor(
                        out=tbuf[:, 1 + off:1 + off + w], in0=ps[:, b, 0:w], in1=sg[:, j, 0:w], op=ALU.mult),
                        [("ps", bxb), ("sg", j)], [("tbuf", sg_) for sg_ in segs])
                if half == 0:
                    act(lambda e, c=c: e.activation(out=tl[:, c:c + 1], in_=tbuf[:, 1024:1025], func=AF.Copy),
                        [("tbuf", "D")], [("tl", c)])
                else:
                    act(lambda e, c=c: e.activation(out=tbuf[:, 0:1], in_=tl[:, c:c + 1], func=AF.Copy),
                        [("tl", c)], [("tbufL",)])
                release(tgc)
                release(txb)
                tk = [("tbuf", sg_) for sg_ in "ABCDE"] + [("tbufL",), ("const", "vecs")]
                ak = [("acc", 0), ("acc", 1)]
                dve(lambda e, c=c: e.tensor_scalar(out=acc[:, :], in0=tbuf[:, 0:1024],
                                                   scalar1=vecs[:, vb + 24 + c:vb + 25 + c], scalar2=None,
                                                   op0=ALU.mult), tk, ak)
                dve(lambda e, c=c: e.scalar_tensor_tensor(out=acc[:, :], in0=tbuf[:, 1:1025],
                                                          scalar=vecs[:, vb + 32 + c:vb + 33 + c], in1=acc[:, :],
                                                          op0=ALU.mult, op1=ALU.add), tk + ak, ak)
                dve(lambda e, c=c: e.scalar_tensor_tensor(out=acc[:, :], in0=tbuf[:, 2:1026],
                                                          scalar=vecs[:, vb + 40 + c:vb + 41 + c], in1=acc[:, :],
                                                          op0=ALU.mult, op1=ALU.add), tk + ak, ak)
                tgb = col("w_in", l, 0, 8, 2048 + c * 128)
                for t in range(2):
                    b = zjob(tgb, t)
                    dve(lambda e, b=b, c=c, t=t: e.tensor_tensor(
                        out=S[:, 16 + c, t * 512:(t + 1) * 512], in0=ps[:, b, :], in1=acc[:, t * 512:(t + 1) * 512],
                        op=ALU.mult), [("ps", b), ("acc", t)], [("S", 16 + c, t)])
                release(tgb)

            for o in range(8):
                tga = col("w_in", l, 0, 8, 5120 + o * 128)
                tgb2 = col("w_in", l, 0, 8, 6144 + o * 128)
                tpa = col("w_proj_a", l, 0, 8, o * 128)
                tpb = col("w_proj_b", l, 0, 8, o * 128)
                for t in range(2):
                    b1 = zjob(tga, t)
                    b2 = zjob(tgb2, t)
                    act(lambda e, b=b1, t=t: e.activation(out=sg[:, t, :], in_=ps[:, b, :], func=AF.Sigmoid),
                        [("ps", b1)], [("sg", t)])
                    act(lambda e, b=b2, t=t: e.activation(out=sg[:, 2 + t, :], in_=ps[:, b, :], func=AF.Sigmoid),
                        [("ps", b2)], [("sg", 2 + t)])
                for t in range(2):
                    ba = sjob(tpa, t, [8 + cc for cc in range(8)])
                    bbk = sjob(tpb, t, [16 + cc for cc in range(8)])
                    dve(lambda e, b=ba, t=t: e.tensor_tensor(out=acc[:, 0:512], in0=ps[:, b, :], in1=sg[:, t, :],
                                                             op=ALU.mult), [("ps", ba), ("sg", t)], [("acc", 0)])
                    dve(lambda e, b=bbk, t=t: e.tensor_tensor(out=acc[:, 512:1024], in0=ps[:, b, :],
                                                              in1=sg[:, 2 + t, :], op=ALU.mult),
                        [("ps", bbk), ("sg", 2 + t)], [("acc", 1)])
                    dve(lambda e, o=o, t=t: e.tensor_tensor(out=S[:, o, t * 512:(t + 1) * 512], in0=acc[:, 0:512],
                                                            in1=acc[:, 512:1024], op=ALU.add),
                        [("acc", 0), ("acc", 1)], [("S", o, t)])
                for t_ in (tga, tgb2, tpa, tpb):
                    release(t_)

            for c in range(8):
                if c in (0, 2, 3, 4, 5, 6):
                    step(nxt)
                tw = col("w_out", l, 0, 8, c * 128)
                for t in range(2):
                    b = sjob(tw, t, list(range(8)))
                    xs = x[:, c, c0 + t * 512:c0 + (t + 1) * 512]
                    dve(lambda e, b=b, xs=xs: e.tensor_tensor(out=xs, in0=ps[:, b, :], in1=xs, op=ALU.add),
                        [("ps", b), ("x", c, 2 * half + t)], [("x", c, 2 * half + t)])
                release(tw)
            drain(nxt)

        def layer_consts(l):
            sc.op("sp", lambda e, l=l: e.dma_start(out=Bb[:, :], in_=bb_d[l, :, :]), (), [("Bb",)], dma_sem="bbs")
            sc.op("pool", lambda e, l=l: e.dma_start(out=wsT[:, :, :], in_=wst_d[l, :, :, :]), (), [("wsT",)],
                  dma_sem="wss")

        def program():
            passes = []
            for l in range(depth):
                for half in range(2):
                    passes.append(("f", l, 1, half, l * 56))
                for half in range(2):
                    passes.append(("m", l, 0, half, l * 56 + 8))
                for half in range(2):
                    passes.append(("f", l, 2, half, l * 56 + 48))
            for half in range(2):
                passes.append(("n", 0, 0, half, DEPTH * 56))
            own = norm_gen(passes[0][4], passes[0][3])
            for i, (kind, l, which, half, vcol) in enumerate(passes):
                if kind == "n":
                    drain(own)
                    own = None
                    if i + 1 < len(passes):
                        own = norm_gen(passes[i + 1][4], passes[i + 1][3], final=True)
                    continue
                nk, _, _, nhalf, nvcol = passes[i + 1]
                nxt = norm_gen(nvcol, nhalf, final=(nk == "n"))
                if kind == "f":
                    if which == 1 and half == 0 and not sc.dry:
                        layer_consts(l)
                    ffn_pass(l, which, half, own, nxt)
                else:
                    mixer_pass(l, half, own, nxt)
                own = None

        sc.dry = True
        program()
        sc.dry = False
        st["bank"] = 0

        sc.op("dve", lambda e: e.memset(ones[:, :], 1.0 / 1024.0), (), [("const", "ones")])
        sc.op("dve", lambda e: e.memset(epsc[:, :], EPS), (), [("const", "eps")])
        sc.op("sp", lambda e: e.dma_start(out=vecs[:, :], in_=vecs_d[:, :]), (), [("const", "vecs")], dma_sem="vl")
        for hf in range(2):
            for k in range(8):
                sc.op("sp", lambda e, k=k, hf=hf: e.dma_start(
                    out=x[:, k, hf * HALF:(hf + 1) * HALF], in_=xT[k * 128:(k + 1) * 128, hf * HALF:(hf + 1) * HALF]),
                    (), [("x", k, 2 * hf), ("x", k, 2 * hf + 1)], dma_sem="xl%d" % (hf * 8 + k))
        pump()
        program()
        assert st["ti"] == len(st["tiles"]) == st["next_dma"]
        for hf in range(2):
            for k in range(8):
                sc.op("sp", lambda e, k=k, hf=hf: e.dma_start(
                    out=yT[k * 128:(k + 1) * 128, hf * HALF:(hf + 1) * HALF], in_=x[:, k, hf * HALF:(hf + 1) * HALF]),
                    [("x", k, 2 * hf), ("x", k, 2 * hf + 1)], (), dma_sem="outs")
        total_out = sc.cnt["outs"]

        with nc.Block() as block:
            @block.sync
            def _(e):
                sc.emit("sp", e)
                e.wait_ge(sc.sem["outs"], total_out)

            @block.gpsimd
            def _(e):
                sc.emit("pool", e)

            @block.tensor
            def _(e):
                sc.emit("pe", e)

            @block.scalar
            def _(e):
                sc.emit("act", e)

            @block.vector
            def _(e):
                sc.emit("dve", e)
    return nc


def host_layout(inputs):
    f = lambda a: np.ascontiguousarray(np.asarray(a, dtype=np.float32))
    vecs = np.zeros((128, NV), np.float32)

    def put(colbase, v):
        vecs[:, colbase:colbase + 8] = np.asarray(v, np.float32).reshape(8, 128).T

    for l in range(DEPTH):
        b = l * 56
        put(b + 0, inputs["ffn1_norm"][l])
        put(b + 8, inputs["mix_norm"][l])
        put(b + 16, inputs["sgu_norm"][l])
        for tap in range(3):
            put(b + 24 + 8 * tap, inputs["conv_w"][l][tap])
        put(b + 48, inputs["ffn2_norm"][l])
    put(DEPTH * 56, inputs["final_norm"])
    sgu_b = np.asarray(inputs["sgu_b"], np.float32)
    bb = np.ascontiguousarray(np.broadcast_to(sgu_b.reshape(DEPTH, 1, 1024), (DEPTH, 128, 1024)))
    sgu_w = np.asarray(inputs["sgu_w"], np.float32)
    wst = np.ascontiguousarray(sgu_w.transpose(0, 3, 1, 2))
    common = {"vecs": vecs, "bb": bb, "wst": wst}
    for name in ("ffn1_w_in", "ffn1_w_out", "w_in", "w_proj_a", "w_proj_b", "w_out", "ffn2_w_in", "ffn2_w_out"):
        common[name] = f(inputs[name])
    return common


_NC_CACHE = {}


def kernel(**inputs):
    x = np.asarray(inputs["x"], np.float32)
    common = host_layout(inputs)
    if "nc" not in _NC_CACHE:
        _NC_CACHE["nc"] = build_nc(DEPTH)
    nc = _NC_CACHE["nc"]
    in_maps = []
    for c in range(NCORES):
        m = dict(common)
        m["xT"] = np.ascontiguousarray(x[c].T)
        in_maps.append(m)
    res = run_bass_kernel_spmd(nc, in_maps, core_ids=list(range(NCORES)))
    out = np.empty((NCORES, SEQ, D), np.float32)
    for c in range(NCORES):
        out[c] = np.asarray(res.results[c]["yT"]).T
    return out
```

```python
import numpy as np
from contextlib import ExitStack

import concourse.bass as bass
import concourse.mybir as mybir
from concourse.bass_utils import run_bass_kernel_spmd

F32 = mybir.dt.float32
BF16 = mybir.dt.bfloat16
AF = mybir.ActivationFunctionType
ALU = mybir.AluOpType

D = 1024
SEQ = 2048
DFF = 2816
NIN = 7168
HALF = 1024
EPS = 1e-6
DEPTH = 4
NCORES = 8
RING = 14
NV = DEPTH * 56 + 8


class Sched:
    def __init__(self, nc, stack):
        self.nc = nc
        self.stack = stack
        self.q = {e: [] for e in ("pe", "act", "dve", "pool", "sp")}
        self.sem = {}
        self.cnt = {}
        self.waited = {}
        self.lastw = {}
        self.readers = {}
        self.dry = False
        self.own = {}
        for e in ("pe", "act", "dve"):
            self.own[e] = self.newsem("c_" + e)

    def newsem(self, name):
        self.sem[name] = self.stack.enter_context(self.nc.semaphore(name))
        self.cnt[name] = 0
        return name

    def _deps(self, eng, reads, writes):
        need = {}
        for k in reads:
            ev = self.lastw.get(k)
            if ev is not None:
                need[ev[0]] = max(need.get(ev[0], 0), ev[1])
        for k in writes:
            ev = self.lastw.get(k)
            if ev is not None:
                need[ev[0]] = max(need.get(ev[0], 0), ev[1])
            for ev in self.readers.get(k, ()):
                need[ev[0]] = max(need.get(ev[0], 0), ev[1])
        waits = []
        for s, v in need.items():
            if self.waited.get((eng, s), 0) < v:
                waits.append((s, v))
                self.waited[(eng, s)] = v
        return waits

    def _record(self, ev, reads, writes):
        for k in writes:
            self.lastw[k] = ev
            self.readers[k] = []
        for k in reads:
            if k[0] == "const":
                continue
            self.readers.setdefault(k, []).append(ev)

    def op(self, eng, fn, reads=(), writes=(), dma_sem=None):
        if self.dry:
            return None
        waits = self._deps(eng, reads, writes)
        if dma_sem is None:
            s = self.own[eng]
            self.cnt[s] += 1
            inc = (s, 1)
        else:
            s = dma_sem
            self.cnt[s] += 16
            inc = (s, 16)
        ev = (s, self.cnt[s])
        self._record(ev, reads, writes)
        self.q[eng].append((waits, fn, inc))
        return ev

    def pe_job(self, mms, bank_key):
        if self.dry:
            return None
        s = self.own["pe"]
        self.cnt[s] += 1
        ev = (s, self.cnt[s])
        allr = []
        n = len(mms)
        for i, (fn, reads) in enumerate(mms):
            waits = self._deps("pe", reads, [bank_key] if i == 0 else [])
            self.q["pe"].append((waits, fn, (s, 1) if i == n - 1 else None))
            allr.extend(reads)
        self._record(ev, set(allr), [bank_key])
        return ev

    def emit(self, eng_name, eng):
        for waits, fn, inc in self.q[eng_name]:
            for s, v in waits:
                eng.wait_ge(self.sem[s], v)
            ins = fn(eng)
            if inc is not None:
                ins.then_inc(self.sem[inc[0]], inc[1])


def build_nc(depth=DEPTH):
    nc = bass.Bass("TRN2", target_bir_lowering=False)

    def din(name, shape):
        return nc.dram_tensor(name, list(shape), F32, kind="ExternalInput").ap()

    xT = din("xT", [D, SEQ])
    vecs_d = din("vecs", [128, NV])
    bb_d = din("bb", [DEPTH, 128, 1024])
    wst_d = din("wst", [DEPTH, 128, 8, 128])
    W = {
        "ffn1_w_in": din("ffn1_w_in", [DEPTH, D, 2 * DFF]),
        "ffn1_w_out": din("ffn1_w_out", [DEPTH, DFF, D]),
        "w_in": din("w_in", [DEPTH, D, NIN]),
        "w_proj_a": din("w_proj_a", [DEPTH, D, D]),
        "w_proj_b": din("w_proj_b", [DEPTH, D, D]),
        "w_out": din("w_out", [DEPTH, D, D]),
        "ffn2_w_in": din("ffn2_w_in", [DEPTH, D, 2 * DFF]),
        "ffn2_w_out": din("ffn2_w_out", [DEPTH, DFF, D]),
    }
    yT = nc.dram_tensor("yT", [D, SEQ], F32, kind="ExternalOutput").ap()

    with ExitStack() as stack:
        def sb(name, shape, dt):
            return stack.enter_context(nc.sbuf_tensor(name, list(shape), dt))

        x = sb("x", [128, 8, SEQ], F32)
        hb = sb("hb", [128, 8, HALF + 8], BF16)
        S = sb("S", [128, 24, HALF], BF16)
        rstd = sb("rstd", [128, HALF], F32)
        ring = sb("ring", [128, RING, 1024], BF16)
        tmpg = sb("tmpg", [128, 2, 512], BF16)
        u_sb = sb("u_sb", [128, 1024], F32)
        tmps = sb("tmps", [128, 512], F32)
        tbuf = sb("tbuf", [128, 1026], F32)
        acc = sb("acc", [128, 1024], F32)
        sg = sb("sg", [128, 4, 512], F32)
        vecs = sb("vecs_sb", [128, NV], F32)
        Bb = sb("Bb", [128, 1024], F32)
        wsT = sb("wsT", [128, 8, 128], BF16)
        ones = sb("ones", [128, 128], BF16)
        ss = sb("ss", [128, 8], F32)
        rs = sb("rs", [128, 8], F32)
        sqh = sb("sqh", [128, 8], BF16)
        tl = sb("tl", [128, 8], F32)
        rsh = sb("rsh", [128, 1], F32)
        epsc = sb("epsc", [128, 1], F32)
        ps = stack.enter_context(nc.psum_tensor("ps", [128, 8, 512], F32))

        sc = Sched(nc, stack)
        for s_ in range(RING):
            sc.newsem("r%d" % s_)
        for k in range(16):
            sc.newsem("xl%d" % k)
        sc.newsem("vl")
        sc.newsem("bbs")
        sc.newsem("wss")
        sc.newsem("outs")

        st = {"bank": 0, "tiles": [], "ti": 0, "next_dma": 0, "released": set()}

        def newbank():
            b = st["bank"] % 8
            st["bank"] += 1
            return b

        act = lambda fn, r=(), w=(): sc.op("act", fn, r, w)
        dve = lambda fn, r=(), w=(): sc.op("dve", fn, r, w)

        def issue_dma(m):
            spec = st["tiles"][m]
            s_ = m % RING
            if spec[0] == "col":
                _, name, l, r0, nk, c0 = spec
                src = W[name][l, r0:r0 + nk * 128, c0:c0 + 128].rearrange("(k p) n -> p k n", p=128)
                dst = ring[:, s_, 0:nk * 128].rearrange("p (k n) -> p k n", k=nk)
            else:
                _, name, l, r0, c0 = spec
                src = W[name][l, r0:r0 + 128, c0:c0 + 1024]
                dst = ring[:, s_, :]
            sc.op("pool", lambda e, dst=dst, src=src: e.dma_start(out=dst, in_=src),
                  reads=(), writes=[("ring", s_)], dma_sem="r%d" % s_)

        def pump():
            if sc.dry:
                return
            n = len(st["tiles"])
            while st["next_dma"] < n:
                m = st["next_dma"]
                if m >= RING and (m - RING) not in st["released"]:
                    break
                issue_dma(m)
                st["next_dma"] += 1

        def tile(spec):
            if sc.dry:
                st["tiles"].append(spec)
                return 0
            m = st["ti"]
            assert st["tiles"][m] == spec
            st["ti"] += 1
            assert m < st["next_dma"], "tile not prefetched (ring too small?)"
            return m % RING, m

        def release(t):
            if sc.dry:
                return
            st["released"].add(t[1])
            pump()

        def col(name, l, r0, nk, c0):
            return tile(("col", name, l, r0, nk, c0))

        def zjob(t_, tt, nk=8):
            b = newbank()
            if sc.dry:
                return b
            s_ = t_[0]
            mms = []
            for k in range(nk):
                mms.append((lambda e, k=k, b=b, s_=s_, tt=tt: e.matmul(
                    ps[:, b, :], lhsT=ring[:, s_, k * 128:(k + 1) * 128],
                    rhs=hb[:, k, tt * 512:(tt + 1) * 512], start=(k == 0), stop=(k == nk - 1)),
                    [("ring", s_), ("h", k, tt)]))
            sc.pe_job(mms, ("ps", b))
            return b

        def zjobw(t_, off, w):
            b = newbank()
            if sc.dry:
                return b
            s_ = t_[0]
            tset = sorted(set(min(cc // 512, 2) for cc in (off, off + w - 1)))
            mms = []
            for k in range(8):
                mms.append((lambda e, k=k, b=b, s_=s_: e.matmul(
                    ps[:, b, 0:w], lhsT=ring[:, s_, k * 128:(k + 1) * 128],
                    rhs=hb[:, k, off:off + w], start=(k == 0), stop=(k == 7)),
                    [("ring", s_)] + [("h", k, tt) for tt in tset]))
            sc.pe_job(mms, ("ps", b))
            return b

        def sjob(t_, tt, rows, nk=8):
            b = newbank()
            if sc.dry:
                return b
            s_ = t_[0]
            mms = []
            for k in range(nk):
                mms.append((lambda e, k=k, b=b, s_=s_, tt=tt: e.matmul(
                    ps[:, b, :], lhsT=ring[:, s_, k * 128:(k + 1) * 128],
                    rhs=S[:, rows[k], tt * 512:(tt + 1) * 512], start=(k == 0), stop=(k == nk - 1)),
                    [("ring", s_), ("S", rows[k], tt)]))
            sc.pe_job(mms, ("ps", b))
            return b

        def xkeys(k, half):
            return [("x", k, 2 * half), ("x", k, 2 * half + 1)]

        def hkeys(k):
            return [("h", k, 0), ("h", k, 1)]

        def norm_gen(vcol, half, final=False):
            c0 = half * HALF
            for k in range(8):
                act(lambda e, k=k: e.activation(out=hb[:, k, 0:HALF], in_=x[:, k, c0:c0 + HALF], func=AF.Square),
                    xkeys(k, half), hkeys(k))
            yield
            for t in range(2):
                b = newbank()
                if not sc.dry:
                    mms = [(lambda e, k=k, b=b, t=t: e.matmul(
                        ps[:, b, :], lhsT=ones[:, :], rhs=hb[:, k, t * 512:(t + 1) * 512],
                        start=(k == 0), stop=(k == 7)), [("h", k, t), ("const", "ones")]) for k in range(8)]
                    sc.pe_job(mms, ("ps", b))
                act(lambda e, b=b, t=t: e.activation(out=rstd[:, t * 512:(t + 1) * 512], in_=ps[:, b, :],
                                                     func=AF.Sqrt, bias=epsc[:, 0:1]), [("ps", b), ("const", "eps")],
                    [("rstd", t)])
                dve(lambda e, t=t: e.reciprocal(out=rstd[:, t * 512:(t + 1) * 512], in_=rstd[:, t * 512:(t + 1) * 512]),
                    [("rstd", t)], [("rstd", t)])
            yield
            for k in range(8):
                if k in (2, 4, 6):
                    yield
                if final:
                    out, wk = x[:, k, c0:c0 + HALF], xkeys(k, half)
                else:
                    out, wk = hb[:, k, 0:HALF], hkeys(k)
                dve(lambda e, k=k, out=out: e.scalar_tensor_tensor(
                    out=out, in0=x[:, k, c0:c0 + HALF], scalar=vecs[:, vcol + k:vcol + k + 1],
                    in1=rstd[:, :], op0=ALU.mult, op1=ALU.mult),
                    xkeys(k, half) + [("rstd", 0), ("rstd", 1), ("const", "vecs")], wk)

        def step(g):
            if g is not None:
                next(g, None)

        def drain(g):
            if g is not None:
                for _ in g:
                    pass

        def ffn_pass(l, which, half, own, nxt):
            c0 = half * HALF
            win = "ffn%d_w_in" % which
            wout = "ffn%d_w_out" % which
            drain(own)
            groups = [(0, 8), (8, 6), (14, 8)]
            slot = 0
            for gi, (j0, nj) in enumerate(groups):
                ab = (gi % 2) * 8
                for jj in range(nj):
                    j = j0 + jj
                    tg = col(win, l, 0, 8, j * 128)
                    tu = col(win, l, 0, 8, DFF + j * 128)
                    for t in range(2):
                        bg = zjob(tg, t)
                        bu = zjob(tu, t)
                        sl = slot % 2
                        slot += 1
                        act(lambda e, bg=bg, sl=sl: e.activation(out=tmpg[:, sl, :], in_=ps[:, bg, :], func=AF.Silu),
                            [("ps", bg)], [("tmpg", sl)])
                        dve(lambda e, bu=bu, sl=sl, r=ab + jj, t=t: e.tensor_tensor(
                            out=S[:, r, t * 512:(t + 1) * 512], in0=ps[:, bu, :], in1=tmpg[:, sl, :], op=ALU.mult),
                            [("ps", bu), ("tmpg", sl)], [("S", ab + jj, t)])
                    release(tg)
                    release(tu)
                for c in range(8):
                    if gi == 2 and c in (0, 2, 3, 4, 5, 6):
                        step(nxt)
                    tw = col(wout, l, j0 * 128, nj, c * 128)
                    for t in range(2):
                        b = sjob(tw, t, [ab + jj for jj in range(nj)], nk=nj)
                        xs = x[:, c, c0 + t * 512:c0 + (t + 1) * 512]
                        dve(lambda e, b=b, xs=xs: e.scalar_tensor_tensor(
                            out=xs, in0=ps[:, b, :], scalar=0.5, in1=xs, op0=ALU.mult, op1=ALU.add),
                            [("ps", b), ("x", c, 2 * half + t)], [("x", c, 2 * half + t)])
                    release(tw)
            drain(nxt)

        def mixer_pass(l, half, own, nxt):
            c0 = half * HALF
            vb = l * 56
            drain(own)

            def halo_norm():
                xc = x[:, :, HALF:HALF + 1].rearrange("p k o -> p (k o)")
                xck = [("x", k, 2) for k in range(8)]
                act(lambda e: e.activation(out=sqh[:, :], in_=xc, func=AF.Square), xck, [("sqh",)])
                b = newbank()
                if not sc.dry:
                    mms = [(lambda e, k=k, b=b: e.matmul(ps[:, b, 0:1], lhsT=ones[:, :], rhs=sqh[:, k:k + 1],
                                                         start=(k == 0), stop=(k == 7)),
                            [("sqh",), ("const", "ones")]) for k in range(8)]
                    sc.pe_job(mms, ("ps", b))
                act(lambda e, b=b: e.activation(out=rsh[:, :], in_=ps[:, b, 0:1], func=AF.Sqrt, bias=epsc[:, 0:1]),
                    [("ps", b), ("const", "eps")], [("rsh",)])
                dve(lambda e: e.reciprocal(out=rsh[:, :], in_=rsh[:, :]), [("rsh",)], [("rsh",)])
                dve(lambda e: e.scalar_tensor_tensor(
                    out=hb[:, :, HALF:HALF + 1].rearrange("p k o -> p (k o)"), in0=xc, scalar=rsh[:, 0:1],
                    in1=vecs[:, vb + 8:vb + 16], op0=ALU.mult, op1=ALU.mult),
                    xck + [("rsh",), ("const", "vecs")], [("h", k, 2) for k in range(8)])
                dve(lambda e: e.memset(tbuf[:, 0:1], 0.0), [], [("tbufL",)])

            wv = [tile(("row", "w_in", l, k * 128, 1024)) for k in range(8)]
            for i in range(8):
                if i == 4:
                    if half == 0:
                        halo_norm()
                    else:
                        dve(lambda e: e.memset(tbuf[:, 1025:1026], 0.0), [], [("tbuf", "E")])
                for fh in range(2):
                    b = newbank()
                    if not sc.dry:
                        mms = [(lambda e, k=k, b=b, i=i, fh=fh, s_=wv[k][0]: e.matmul(
                            ps[:, b, :], lhsT=hb[:, k, i * 128:(i + 1) * 128],
                            rhs=ring[:, s_, fh * 512:(fh + 1) * 512], start=(k == 0), stop=(k == 7)),
                            [("ring", wv[k][0]), ("h", k, i // 4)]) for k in range(8)]
                        sc.pe_job(mms, ("ps", b))
                    act(lambda e, b=b, fh=fh, i=i: e.activation(out=S[:, i, fh * 512:(fh + 1) * 512], in_=ps[:, b, :],
                                                                func=AF.Gelu_apprx_tanh), [("ps", b)], [("S", i, fh)])
                dve(lambda e, i=i: e.scalar_tensor_tensor(
                    out=tmpg[:, :, :].rearrange("p a n -> p (a n)"), in0=S[:, i, :], scalar=1.0, in1=S[:, i, :],
                    op0=ALU.mult, op1=ALU.mult, accum_out=ss[:, i:i + 1]),
                    [("S", i, 0), ("S", i, 1), ("ss",)], [("tmpg", 0), ("tmpg", 1), ("ss",)])
            act(lambda e: e.activation(out=rs[:, :], in_=ss[:, :], func=AF.Sqrt, bias=epsc[:, 0:1],
                                       scale=1.0 / 1024.0), [("ss",), ("const", "eps")], [("rs",)])
            dve(lambda e: e.reciprocal(out=rs[:, :], in_=rs[:, :]), [("rs",)], [("rs",)])
            for i in range(8):
                dve(lambda e, i=i: e.tensor_scalar(out=S[:, i, :], in0=S[:, i, :], scalar1=rs[:, i:i + 1],
                                                   scalar2=None, op0=ALU.mult),
                    [("S", i, 0), ("S", i, 1), ("rs",)], [("S", i, 0), ("S", i, 1)])
            for t_ in wv:
                release(t_)

            for c in range(8):
                tu = col("w_in", l, 0, 8, c * 128)
                for t in range(2):
                    b = zjob(tu, t)
                    act(lambda e, b=b, t=t: e.activation(out=u_sb[:, t * 512:(t + 1) * 512], in_=ps[:, b, :],
                                                         func=AF.Gelu_apprx_tanh), [("ps", b)], [("u", t)])
                release(tu)
                tgc = col("w_in", l, 0, 8, 3072 + c * 128)
                txb = col("w_in", l, 0, 8, 4096 + c * 128)
                if half == 0:
                    wins = [(0, 342, ["A"]), (342, 342, ["B", "C"]), (684, 341, ["D", "E"])]
                else:
                    wins = [(0, 512, ["A", "B"]), (512, 512, ["C", "D"])]
                for j, (off, w, segs) in enumerate(wins):
                    bgc = zjobw(tgc, off, w)
                    bxb = zjobw(txb, off, w)
                    act(lambda e, b=bgc, j=j, w=w: e.activation(out=sg[:, j, 0:w], in_=ps[:, b, 0:w], func=AF.Copy),
                        [("ps", bgc)], [("sg", j)])
                    dve(lambda e, b=bxb, j=j, off=off, w=w: e.tensor_tensor(
                        out=tbuf[:, 1 + off:1 + off + w], in0=ps[:, b, 0:w], in1=sg[:, j, 0:w], op=ALU.mult),
                        [("ps", bxb), ("sg", j)], [("tbuf", sg_) for sg_ in segs])
                if half == 0:
                    act(lambda e, c=c: e.activation(out=tl[:, c:c + 1], in_=tbuf[:, 1024:1025], func=AF.Copy),
                        [("tbuf", "D")], [("tl", c)])
                else:
                    act(lambda e, c=c: e.activation(out=tbuf[:, 0:1], in_=tl[:, c:c + 1], func=AF.Copy),
                        [("tl", c)], [("tbufL",)])
                release(tgc)
                release(txb)
                for t in range(2):
                    b = newbank()
                    if not sc.dry:
                        mms = [(lambda e, ii=ii, b=b, t=t, c=c: e.matmul(
                            ps[:, b, ii * 128:(ii + 1) * 128], lhsT=S[:, 4 * t + ii, c * 128:(c + 1) * 128],
                            rhs=wsT[:, c, :], start=True, stop=True),
                            [("S", 4 * t + ii, c // 4), ("wsT",)]) for ii in range(4)]
                        sc.pe_job(mms, ("ps", b))
                    dve(lambda e, b=b, c=c: e.scalar_tensor_tensor(
                        out=tmps[:, :].rearrange("p (a q) -> p a q", a=4),
                        in0=ps[:, b, :].rearrange("p (a q) -> p a q", a=4),
                        scalar=vecs[:, vb + 16 + c:vb + 17 + c],
                        in1=Bb[:, c * 128:(c + 1) * 128].unsqueeze(1).broadcast_to([128, 4, 128]),
                        op0=ALU.mult, op1=ALU.add),
                        [("ps", b), ("Bb",), ("const", "vecs")], [("tmps",)])
                    dve(lambda e, c=c, t=t: e.tensor_tensor(
                        out=S[:, 8 + c, t * 512:(t + 1) * 512], in0=tmps[:, :], in1=u_sb[:, t * 512:(t + 1) * 512],
                        op=ALU.mult), [("tmps",), ("u", t)], [("S", 8 + c, t)])
                tk = [("tbuf", sg_) for sg_ in "ABCDE"] + [("tbufL",), ("const", "vecs")]
                ak = [("acc", 0), ("acc", 1)]
                dve(lambda e, c=c: e.tensor_scalar(out=acc[:, :], in0=tbuf[:, 0:1024],
                                                   scalar1=vecs[:, vb + 24 + c:vb + 25 + c], scalar2=None,
                                                   op0=ALU.mult), tk, ak)
                dve(lambda e, c=c: e.scalar_tensor_tensor(out=acc[:, :], in0=tbuf[:, 1:1025],
                                                          scalar=vecs[:, vb + 32 + c:vb + 33 + c], in1=acc[:, :],
                                                          op0=ALU.mult, op1=ALU.add), tk + ak, ak)
                dve(lambda e, c=c: e.scalar_tensor_tensor(out=acc[:, :], in0=tbuf[:, 2:1026],
                                                          scalar=vecs[:, vb + 40 + c:vb + 41 + c], in1=acc[:, :],
                                                          op0=ALU.mult, op1=ALU.add), tk + ak, ak)
                tgb = col("w_in", l, 0, 8, 2048 + c * 128)
                for t in range(2):
                    b = zjob(tgb, t)
                    dve(lambda e, b=b, c=c, t=t: e.tensor_tensor(
                        out=S[:, 16 + c, t * 512:(t + 1) * 512], in0=ps[:, b, :], in1=acc[:, t * 512:(t + 1) * 512],
                        op=ALU.mult), [("ps", b), ("acc", t)], [("S", 16 + c, t)])
                release(tgb)

            for o in range(8):
                tga = col("w_in", l, 0, 8, 5120 + o * 128)
                tgb2 = col("w_in", l, 0, 8, 6144 + o * 128)
                tpa = col("w_proj_a", l, 0, 8, o * 128)
                tpb = col("w_proj_b", l, 0, 8, o * 128)
                for t in range(2):
                    b1 = zjob(tga, t)
                    b2 = zjob(tgb2, t)
                    act(lambda e, b=b1, t=t: e.activation(out=sg[:, t, :], in_=ps[:, b, :], func=AF.Sigmoid),
                        [("ps", b1)], [("sg", t)])
                    act(lambda e, b=b2, t=t: e.activation(out=sg[:, 2 + t, :], in_=ps[:, b, :], func=AF.Sigmoid),
                        [("ps", b2)], [("sg", 2 + t)])
                for t in range(2):
                    ba = sjob(tpa, t, [8 + cc for cc in range(8)])
                    bbk = sjob(tpb, t, [16 + cc for cc in range(8)])
                    dve(lambda e, b=ba, t=t: e.tensor_tensor(out=acc[:, 0:512], in0=ps[:, b, :], in1=sg[:, t, :],
                                                             op=ALU.mult), [("ps", ba), ("sg", t)], [("acc", 0)])
                    dve(lambda e, b=bbk, t=t: e.tensor_tensor(out=acc[:, 512:1024], in0=ps[:, b, :],
                                                              in1=sg[:, 2 + t, :], op=ALU.mult),
                        [("ps", bbk), ("sg", 2 + t)], [("acc", 1)])
                    dve(lambda e, o=o, t=t: e.tensor_tensor(out=S[:, o, t * 512:(t + 1) * 512], in0=acc[:, 0:512],
                                                            in1=acc[:, 512:1024], op=ALU.add),
                        [("acc", 0), ("acc", 1)], [("S", o, t)])
                for t_ in (tga, tgb2, tpa, tpb):
                    release(t_)

            for c in range(8):
                if c in (0, 2, 3, 4, 5, 6):
                    step(nxt)
                tw = col("w_out", l, 0, 8, c * 128)
                for t in range(2):
                    b = sjob(tw, t, list(range(8)))
                    xs = x[:, c, c0 + t * 512:c0 + (t + 1) * 512]
                    dve(lambda e, b=b, xs=xs: e.tensor_tensor(out=xs, in0=ps[:, b, :], in1=xs, op=ALU.add),
                        [("ps", b), ("x", c, 2 * half + t)], [("x", c, 2 * half + t)])
                release(tw)
            drain(nxt)

        def layer_consts(l):
            sc.op("sp", lambda e, l=l: e.dma_start(out=Bb[:, :], in_=bb_d[l, :, :]), (), [("Bb",)], dma_sem="bbs")
            sc.op("pool", lambda e, l=l: e.dma_start(out=wsT[:, :, :], in_=wst_d[l, :, :, :]), (), [("wsT",)],
                  dma_sem="wss")

        def program():
            passes = []
            for l in range(depth):
                for half in range(2):
                    passes.append(("f", l, 1, half, l * 56))
                for half in range(2):
                    passes.append(("m", l, 0, half, l * 56 + 8))
                for half in range(2):
                    passes.append(("f", l, 2, half, l * 56 + 48))
            for half in range(2):
                passes.append(("n", 0, 0, half, DEPTH * 56))
            own = norm_gen(passes[0][4], passes[0][3])
            for i, (kind, l, which, half, vcol) in enumerate(passes):
                if kind == "n":
                    drain(own)
                    own = None
                    if i + 1 < len(passes):
                        own = norm_gen(passes[i + 1][4], passes[i + 1][3], final=True)
                    continue
                nk, _, _, nhalf, nvcol = passes[i + 1]
                nxt = norm_gen(nvcol, nhalf, final=(nk == "n"))
                if kind == "f":
                    if which == 1 and half == 0 and not sc.dry:
                        layer_consts(l)
                    ffn_pass(l, which, half, own, nxt)
                else:
                    mixer_pass(l, half, own, nxt)
                own = None

        sc.dry = True
        program()
        sc.dry = False
        st["bank"] = 0

        sc.op("dve", lambda e: e.memset(ones[:, :], 1.0 / 1024.0), (), [("const", "ones")])
        sc.op("dve", lambda e: e.memset(epsc[:, :], EPS), (), [("const", "eps")])
        sc.op("sp", lambda e: e.dma_start(out=vecs[:, :], in_=vecs_d[:, :]), (), [("const", "vecs")], dma_sem="vl")
        for hf in range(2):
            for k in range(8):
                sc.op("sp", lambda e, k=k, hf=hf: e.dma_start(
                    out=x[:, k, hf * HALF:(hf + 1) * HALF], in_=xT[k * 128:(k + 1) * 128, hf * HALF:(hf + 1) * HALF]),
                    (), [("x", k, 2 * hf), ("x", k, 2 * hf + 1)], dma_sem="xl%d" % (hf * 8 + k))
        pump()
        program()
        assert st["ti"] == len(st["tiles"]) == st["next_dma"]
        for hf in range(2):
            for k in range(8):
                sc.op("sp", lambda e, k=k, hf=hf: e.dma_start(
                    out=yT[k * 128:(k + 1) * 128, hf * HALF:(hf + 1) * HALF], in_=x[:, k, hf * HALF:(hf + 1) * HALF]),
                    [("x", k, 2 * hf), ("x", k, 2 * hf + 1)], (), dma_sem="outs")
        total_out = sc.cnt["outs"]

        with nc.Block() as block:
            @block.sync
            def _(e):
                sc.emit("sp", e)
                e.wait_ge(sc.sem["outs"], total_out)

            @block.gpsimd
            def _(e):
                sc.emit("pool", e)

            @block.tensor
            def _(e):
                sc.emit("pe", e)

            @block.scalar
            def _(e):
                sc.emit("act", e)

            @block.vector
            def _(e):
                sc.emit("dve", e)
    return nc


def host_layout(inputs):
    f = lambda a: np.ascontiguousarray(np.asarray(a, dtype=np.float32))
    vecs = np.zeros((128, NV), np.float32)

    def put(colbase, v):
        vecs[:, colbase:colbase + 8] = np.asarray(v, np.float32).reshape(8, 128).T

    for l in range(DEPTH):
        b = l * 56
        put(b + 0, inputs["ffn1_norm"][l])
        put(b + 8, inputs["mix_norm"][l])
        put(b + 16, inputs["sgu_norm"][l])
        for tap in range(3):
            put(b + 24 + 8 * tap, inputs["conv_w"][l][tap])
        put(b + 48, inputs["ffn2_norm"][l])
    put(DEPTH * 56, inputs["final_norm"])
    sgu_b = np.asarray(inputs["sgu_b"], np.float32)
    bb = np.ascontiguousarray(np.broadcast_to(sgu_b.reshape(DEPTH, 1, 1024), (DEPTH, 128, 1024)))
    sgu_w = np.asarray(inputs["sgu_w"], np.float32)
    wst = np.ascontiguousarray(sgu_w.transpose(0, 3, 1, 2))
    common = {"vecs": vecs, "bb": bb, "wst": wst}
    for name in ("ffn1_w_in", "ffn1_w_out", "w_in", "w_proj_a", "w_proj_b", "w_out", "ffn2_w_in", "ffn2_w_out"):
        common[name] = f(inputs[name])
    return common


_NC_CACHE = {}


def kernel(**inputs):
    x = np.asarray(inputs["x"], np.float32)
    common = host_layout(inputs)
    if "nc" not in _NC_CACHE:
        _NC_CACHE["nc"] = build_nc(DEPTH)
    nc = _NC_CACHE["nc"]
    in_maps = []
    for c in range(NCORES):
        m = dict(common)
        m["xT"] = np.ascontiguousarray(x[c].T)
        in_maps.append(m)
    res = run_bass_kernel_spmd(nc, in_maps, core_ids=list(range(NCORES)))
    out = np.empty((NCORES, SEQ, D), np.float32)
    for c in range(NCORES):
        out[c] = np.asarray(res.results[c]["yT"]).T
    return out
```
